# Optimizing a Trainium2 kernel written in Bass

```python
import jax, jax.numpy as jnp
from jax import lax
import numpy as np

D_MODEL = 1024
BATCH = 1
SEQ = 16384
DEPTH = 4

N_MIXERS = 3
D_FF = 4 * D_MODEL
Q_BLOCK = 128
ROPE_BASE = 10000.0
EPS = 1e-6
NEG_INF = -1e30

MLA_HEADS = 8
MLA_Q_RANK = 256
MLA_KV_RANK = 128
MLA_NOPE = 128
MLA_ROPE = 64
MLA_V = 128

SB_HEADS = 16
SB_HEAD_DIM = D_MODEL // SB_HEADS

RET_HEADS = 4
RET_KEY_DIM = D_MODEL // RET_HEADS
RET_VAL_DIM = 2 * D_MODEL // RET_HEADS
RET_CHUNK = 128

N_MLA_LAYERS = (DEPTH + 2) // 3
N_SB_LAYERS = (DEPTH + 1) // 3
N_RET_LAYERS = DEPTH // 3

kernel_name = "hybrid_mla_stickbreak_retention_trunk"


def rms_norm(x, g):
    xf = x.astype(jnp.float32)
    y = xf * lax.rsqrt(jnp.mean(xf * xf, axis=-1, keepdims=True) + EPS)
    return (y * g.astype(jnp.float32)).astype(x.dtype)


def modulate(x, shift, scale):
    return x * (1.0 + scale[:, None, :]) + shift[:, None, :]


def rope_tables(positions, dim):
    inv_freq = ROPE_BASE ** (-jnp.arange(0, dim, 2, dtype=jnp.float32) / dim)
    ang = positions.astype(jnp.float32)[..., None] * inv_freq
    return jnp.cos(ang)[:, :, None, :], jnp.sin(ang)[:, :, None, :]


def apply_rope(t, cos, sin):
    t1, t2 = jnp.split(t, 2, axis=-1)
    cos = cos.astype(t.dtype)
    sin = sin.astype(t.dtype)
    return jnp.concatenate([t1 * cos - t2 * sin, t2 * cos + t1 * sin], axis=-1)


def to_blocks(t, size):
    b, s, h, d = t.shape
    return t.reshape(b, s // size, size, h, d).transpose(1, 0, 2, 3, 4)


def from_blocks(t):
    nb, b, size, h, d = t.shape
    return t.transpose(1, 0, 2, 3, 4).reshape(b, nb * size, h, d)


def blocked_causal_softmax(q, k, v, scale):
    k_pos = jnp.arange(k.shape[1])
    qb = to_blocks(q, Q_BLOCK)

    def one_block(args):
        q_blk, b_idx = args
        q_pos = b_idx * Q_BLOCK + jnp.arange(Q_BLOCK)
        s = jnp.einsum('bqhd,bkhd->bhqk', q_blk, k).astype(jnp.float32) * scale
        s = jnp.where(k_pos[None, :] <= q_pos[:, None], s, NEG_INF)
        p = jax.nn.softmax(s, axis=-1).astype(v.dtype)
        return jnp.einsum('bhqk,bkhd->bqhd', p, v)

    o = lax.map(one_block, (qb, jnp.arange(qb.shape[0])))
    return from_blocks(o)


def mla_mixer(h, positions, w_in, q_norm, kv_norm, w_q_up, w_kv_up, w_out):
    b, s, _ = h.shape
    q_lat, kv_lat, k_pe = jnp.split(h @ w_in, [MLA_Q_RANK, MLA_Q_RANK + MLA_KV_RANK], axis=-1)
    cos, sin = rope_tables(positions, MLA_ROPE)
    q = (rms_norm(q_lat, q_norm) @ w_q_up).reshape(b, s, MLA_HEADS, MLA_NOPE + MLA_ROPE)
    q_nope, q_pe = jnp.split(q, [MLA_NOPE], axis=-1)
    q = jnp.concatenate([q_nope, apply_rope(q_pe, cos, sin)], axis=-1)
    kv = (rms_norm(kv_lat, kv_norm) @ w_kv_up).reshape(b, s, MLA_HEADS, MLA_NOPE + MLA_V)
    k_nope, v = jnp.split(kv, [MLA_NOPE], axis=-1)
    k_pe = apply_rope(k_pe[:, :, None, :], cos, sin)
    k = jnp.concatenate([k_nope, jnp.broadcast_to(k_pe, (b, s, MLA_HEADS, MLA_ROPE))], axis=-1)
    o = blocked_causal_softmax(q, k, v, (MLA_NOPE + MLA_ROPE) ** -0.5)
    return o.reshape(b, s, MLA_HEADS * MLA_V) @ w_out


def stick_breaking_mixer(h, w_in, w_out):
    b, s, _ = h.shape
    q, k, v = jnp.split(h @ w_in, 3, axis=-1)
    q = q.reshape(b, s, SB_HEADS, SB_HEAD_DIM)
    k = k.reshape(b, s, SB_HEADS, SB_HEAD_DIM)
    v = v.reshape(b, s, SB_HEADS, SB_HEAD_DIM)
    scale = SB_HEAD_DIM ** -0.5
    k_pos = jnp.arange(s)
    qb = to_blocks(q, Q_BLOCK)

    def one_block(args):
        q_blk, b_idx = args
        q_pos = b_idx * Q_BLOCK + jnp.arange(Q_BLOCK)
        z = jnp.einsum('bqhd,bkhd->bhqk', q_blk, k).astype(jnp.float32) * scale
        strict = k_pos[None, :] < q_pos[:, None]
        log_1m = jnp.where(strict, jax.nn.log_sigmoid(-z), 0.0)
        suffix = lax.cumsum(log_1m, axis=3, reverse=True) - log_1m
        a = jnp.where(strict, jnp.exp(jax.nn.log_sigmoid(z) + suffix), 0.0).astype(v.dtype)
        return jnp.einsum('bhqk,bkhd->bqhd', a, v)

    o = from_blocks(lax.map(one_block, (qb, jnp.arange(qb.shape[0]))))
    return o.reshape(b, s, SB_HEADS * SB_HEAD_DIM) @ w_out


def retention_mixer(h, positions, w_in, gn_g, w_out):
    b, s, _ = h.shape
    hk = RET_HEADS * RET_KEY_DIM
    hv = RET_HEADS * RET_VAL_DIM
    q, k, v, g = jnp.split(h @ w_in, [hk, 2 * hk, 2 * hk + hv], axis=-1)
    cos, sin = rope_tables(positions, RET_KEY_DIM)
    q = apply_rope(q.reshape(b, s, RET_HEADS, RET_KEY_DIM), cos, sin).astype(jnp.float32)
    k = apply_rope(k.reshape(b, s, RET_HEADS, RET_KEY_DIM), cos, sin).astype(jnp.float32) * (RET_KEY_DIM ** -0.5)
    v = v.reshape(b, s, RET_HEADS, RET_VAL_DIM).astype(jnp.float32)

    log_gamma = jnp.log(1.0 - 2.0 ** (-5.0 - jnp.arange(RET_HEADS, dtype=jnp.float32)))
    idx = jnp.arange(RET_CHUNK, dtype=jnp.float32)
    diff = idx[:, None] - idx[None, :]
    decay = jnp.where(diff[None] >= 0, jnp.exp(jnp.maximum(diff, 0.0)[None] * log_gamma[:, None, None]), 0.0)
    xi = jnp.exp((idx[:, None] + 1.0) * log_gamma[None, :])
    zeta = jnp.exp((RET_CHUNK - 1.0 - idx)[:, None] * log_gamma[None, :])
    g_chunk = jnp.exp(RET_CHUNK * log_gamma)

    def step(state, inp):
        qc, kc, vc = inp
        sc = jnp.einsum('bihd,bjhd->bhij', qc, kc) * decay[None]
        inner = jnp.einsum('bhij,bjhv->bihv', sc, vc)
        cross = jnp.einsum('bihd,bhdv->bihv', qc, state) * xi[None, :, :, None]
        state = state * g_chunk[None, :, None, None] + jnp.einsum('bjhd,bjhv->bhdv', kc * zeta[None, :, :, None], vc)
        return state, inner + cross

    state0 = jnp.zeros((b, RET_HEADS, RET_KEY_DIM, RET_VAL_DIM), jnp.float32)
    _, o = lax.scan(step, state0, (to_blocks(q, RET_CHUNK), to_blocks(k, RET_CHUNK), to_blocks(v, RET_CHUNK)))
    o = from_blocks(o)
    o = o * lax.rsqrt(jnp.mean(o * o, axis=-1, keepdims=True) + EPS)
    o = o.reshape(b, s, hv) * gn_g.astype(jnp.float32)
    o = (jax.nn.silu(g.astype(jnp.float32)) * o).astype(h.dtype)
    return o @ w_out


def setup_inputs(seed: int = 0) -> dict:
    key = jax.random.key(seed)
    ks = jax.random.split(key, 24)

    def dense(k, shape, fan_in, gain=1.0):
        return jax.random.normal(k, shape, jnp.float32) * (gain * fan_in ** -0.5)

    def gains(k, shape):
        return 1.0 + 0.05 * jax.random.normal(k, shape, jnp.float32)

    hk = RET_HEADS * RET_KEY_DIM
    hv = RET_HEADS * RET_VAL_DIM
    return {
        "x": jax.random.normal(ks[0], (BATCH, SEQ, D_MODEL), jnp.float32),
        "c": jax.random.normal(ks[1], (BATCH, D_MODEL), jnp.float32),
        "positions": jnp.broadcast_to(jnp.arange(SEQ, dtype=jnp.int32), (BATCH, SEQ)),
        "ada_w": dense(ks[2], (DEPTH, D_MODEL, 6 * D_MODEL), D_MODEL, 0.5),
        "ada_b": 0.01 * jax.random.normal(ks[3], (DEPTH, 6 * D_MODEL), jnp.float32),
        "norm_g": gains(ks[4], (DEPTH, 4, D_MODEL)),
        "ffn_w1": dense(ks[5], (DEPTH, D_MODEL, D_FF), D_MODEL),
        "ffn_w2": dense(ks[6], (DEPTH, D_FF, D_MODEL), D_FF),
        "mla_w_in": dense(ks[7], (N_MLA_LAYERS, D_MODEL, MLA_Q_RANK + MLA_KV_RANK + MLA_ROPE), D_MODEL),
        "mla_q_norm": gains(ks[8], (N_MLA_LAYERS, MLA_Q_RANK)),
        "mla_kv_norm": gains(ks[9], (N_MLA_LAYERS, MLA_KV_RANK)),
        "mla_w_q_up": dense(ks[10], (N_MLA_LAYERS, MLA_Q_RANK, MLA_HEADS * (MLA_NOPE + MLA_ROPE)), MLA_Q_RANK),
        "mla_w_kv_up": dense(ks[11], (N_MLA_LAYERS, MLA_KV_RANK, MLA_HEADS * (MLA_NOPE + MLA_V)), MLA_KV_RANK),
        "mla_w_out": dense(ks[12], (N_MLA_LAYERS, MLA_HEADS * MLA_V, D_MODEL), MLA_HEADS * MLA_V),
        "sb_w_in": dense(ks[13], (N_SB_LAYERS, D_MODEL, 3 * SB_HEADS * SB_HEAD_DIM), D_MODEL),
        "sb_w_out": dense(ks[14], (N_SB_LAYERS, SB_HEADS * SB_HEAD_DIM, D_MODEL), SB_HEADS * SB_HEAD_DIM),
        "ret_w_in": dense(ks[15], (N_RET_LAYERS, D_MODEL, 2 * hk + 2 * hv), D_MODEL),
        "ret_gn_g": gains(ks[16], (N_RET_LAYERS, hv)),
        "ret_w_out": dense(ks[17], (N_RET_LAYERS, hv, D_MODEL), hv),
    }


def reference(x, c, positions, ada_w, ada_b, norm_g, ffn_w1, ffn_w2,
              mla_w_in, mla_q_norm, mla_kv_norm, mla_w_q_up, mla_w_kv_up, mla_w_out,
              sb_w_in, sb_w_out, ret_w_in, ret_gn_g, ret_w_out):
    cond = jax.nn.silu(c)
    for i in range(DEPTH):
        mod = cond @ ada_w[i] + ada_b[i]
        sh_a, sc_a, g_a, sh_f, sc_f, g_f = jnp.split(mod, 6, axis=-1)

        h = modulate(rms_norm(x, norm_g[i, 0]), sh_a, sc_a)
        kind, j = i % N_MIXERS, i // N_MIXERS
        if kind == 0:
            y = mla_mixer(h, positions, mla_w_in[j], mla_q_norm[j], mla_kv_norm[j],
                          mla_w_q_up[j], mla_w_kv_up[j], mla_w_out[j])
        elif kind == 1:
            y = stick_breaking_mixer(h, sb_w_in[j], sb_w_out[j])
        else:
            y = retention_mixer(h, positions, ret_w_in[j], ret_gn_g[j], ret_w_out[j])
        x = x + g_a[:, None, :] * rms_norm(y, norm_g[i, 1])

        h = modulate(rms_norm(x, norm_g[i, 2]), sh_f, sc_f)
        y = jnp.square(jax.nn.relu(h @ ffn_w1[i])) @ ffn_w2[i]
        x = x + g_f[:, None, :] * rms_norm(y, norm_g[i, 3])
    return x
```

```python
import numpy as np
from contextlib import ExitStack
import concourse.bass as bass
import concourse.mybir as mybir
from concourse.bass_utils import run_bass_kernel_spmd

F32, BF16, I32 = mybir.dt.float32, mybir.dt.bfloat16, mybir.dt.int32
AF = mybir.ActivationFunctionType
ALU = mybir.AluOpType
AX = mybir.AxisListType
NCORES = 8
D = 1024
DFF = 4096
EPS = 1e-6
P = 128


class Buf:
    __slots__ = ("name", "lw", "rd", "sem", "cnt")

    def __init__(self, name):
        self.name, self.lw, self.rd, self.sem, self.cnt = name, {}, {}, None, 0


class T:
    def __init__(self, t, b):
        self.t, self.b = t, b

    def __getitem__(self, k):
        return self.t[k]


class K:
    ENG = ("pe", "act", "dve", "pool", "sp")

    def __init__(self):
        self.nc = bass.Bass("TRN2", target_bir_lowering=False)
        self.es = ExitStack()
        self.ops = {e: [] for e in self.ENG}
        self.cnt = {e: 0 for e in self.ENG}
        self.seen = {e: {} for e in self.ENG}
        self.semh = {}
        for e in ("pe", "act", "dve", "pool"):
            self.semh[e] = self.es.enter_context(self.nc.semaphore("sem_" + e))
        self.nsem = 4
        self.uid = 0
        self.outs = []
        self.banks = []

    def sb(self, name, shape, dtype, dma=False):
        t = self.es.enter_context(self.nc.sbuf_tensor("s_" + name, list(shape), dtype))
        b = Buf(name)
        if dma:
            self.dmasem(b)
        return T(t, b)

    def sub(self, name, dma=False):
        b = Buf(name)
        if dma:
            self.dmasem(b)
        return b

    def dmasem(self, b):
        if b.sem is None:
            key = "d_" + b.name + str(self.nsem)
            self.semh[key] = self.es.enter_context(self.nc.semaphore(key))
            b.sem = key
            self.nsem += 1
        return b

    def psum_banks(self, n=8):
        for i in range(n):
            t = self.es.enter_context(self.nc.psum_tensor("bank%d" % i, [P, 512], F32))
            self.banks.append(T(t, Buf("bank%d" % i)))
        return self.banks

    def dram(self, name, shape, dtype, kind):
        t = self.nc.dram_tensor(name, list(shape), dtype, kind=kind)
        tt = T(t.ap(), self.dmasem(Buf(name)))
        if kind == "ExternalOutput":
            self.outs.append(tt)
        return tt

    def op(self, eng, fn, R=(), W=(), dmab=None):
        me = eng if eng in self.semh else None
        need = {}

        def merge(d, raw):
            for k, c in d.items():
                if k == me and (eng == "pe" or not raw):
                    continue
                if need.get(k, 0) < c:
                    need[k] = c

        for b in R:
            merge(b.lw, True)
        for b in W:
            merge(b.lw, False)
            merge(b.rd, False)
        waits = []
        seen = self.seen[eng]
        for k, c in need.items():
            if seen.get(k, 0) >= c:
                continue
            seen[k] = c
            waits.append((k, c))
        if dmab is not None:
            dmab.cnt += 16
            key, c, inc = dmab.sem, dmab.cnt, 16
        else:
            self.cnt[eng] += 1
            key, c, inc = me, self.cnt[eng], 1
        self.ops[eng].append((waits, fn, key, inc))
        for b in R:
            if b.rd.get(key, 0) < c:
                b.rd[key] = c
        for b in W:
            b.lw = {key: c}
            b.rd = {}

    def mm(self, out, lhsT, rhs, start, stop, R, W, skip=False):
        self.op("pe", lambda e: e.matmul(out, lhsT, rhs, start=start, stop=stop,
                                          skip_group_check=skip), R, W)

    def tr(self, out, in_, ident, R, W):
        self.op("pe", lambda e: e.transpose(out, in_, ident), R, W)

    def act(self, out, in_, func, R, W, bias=None, scale=None, accum=None):
        kw = {}
        if bias is not None:
            kw["bias"] = bias
        if scale is not None:
            kw["scale"] = scale
        if accum is not None:
            kw["accum_out"] = accum
        self.op("act", lambda e: e.activation(out, in_, func, **kw), R, W)

    def tt(self, eng, out, in0, in1, op, R, W):
        self.op(eng, lambda e: e.tensor_tensor(out, in0, in1, op), R, W)

    def ts(self, eng, out, in0, s1, s2, op0, op1, R, W):
        if s2 is None:
            self.op(eng, lambda e: e.tensor_scalar(out, in0, s1, None, op0), R, W)
        else:
            self.op(eng, lambda e: e.tensor_scalar(out, in0, s1, s2, op0, op1), R, W)

    def stt(self, eng, out, in0, scalar, in1, op0, op1, R, W):
        self.op(eng, lambda e: e.scalar_tensor_tensor(out, in0, scalar, in1, op0, op1), R, W)

    def copy(self, eng, out, in_, R, W):
        if eng == "act":
            self.op(eng, lambda e: e.copy(out, in_), R, W)
        else:
            self.op(eng, lambda e: e.tensor_copy(out, in_), R, W)

    def memset(self, eng, ap, val, W):
        self.op(eng, lambda e: e.memset(ap, val), (), W)

    def recip(self, out, in_, R, W):
        self.op("dve", lambda e: e.reciprocal(out, in_), R, W)

    def dma(self, q, out, in_, R, W, semb, **kw):
        self.op(q, lambda e: e.dma_start(out=out, in_=in_, **kw), R, W, dmab=semb)

    def load(self, q, dst, dst_ap, src, src_ap, **kw):
        self.dma(q, dst_ap, src_ap, [src.b], [dst.b], dst.b, **kw)

    def finish(self):
        for o in self.outs:
            if o.b.cnt:
                self.ops["sp"].append(([(o.b.sem, o.b.cnt)], None, None, 0))
        nc = self.nc
        with nc.Block() as block:
            def run(name):
                def body(e):
                    for waits, fn, key, inc in self.ops[name]:
                        for k, c in waits:
                            e.wait_ge(self.semh[k], c)
                        if fn is not None:
                            fn(e).then_inc(self.semh[key], inc)
                return body
            block.tensor(run("pe"))
            block.scalar(run("act"))
            block.vector(run("dve"))
            block.gpsimd(run("pool"))
            block.sync(run("sp"))
        self.es.close()
        return nc


def run_spmd(k, in_maps):
    nc = k.finish()
    res = run_bass_kernel_spmd(nc, in_maps, core_ids=list(range(NCORES)))
    return res.results


def bcast_load(k, q, name, src, src_ap_row):
    n = src_ap_row.shape[-1]
    t = k.sb(name, [P, n], F32, dma=True)
    k.load(q, t, t[:], src, src_ap_row.partition_broadcast(P))
    return t


def emit_rstd(k, ss, rstd, n, tmpb):
    k.ts("dve", rstd[:], ss[:], 1.0 / n, EPS, ALU.mult, ALU.add, [ss.b], [rstd.b])
    k.act(rstd[:], rstd[:], AF.Sqrt, [rstd.b], [rstd.b])
    k.recip(rstd[:], rstd[:], [rstd.b], [rstd.b])


class Common:
    def __init__(self, k, consts):
        self.k = k
        self.ident = k.sb("ident", [P, P], BF16, dma=True)
        k.load("pool", self.ident, self.ident[:], consts, consts[:, 0:P])
        self.junk = k.sb("junk", [P, D], BF16)
        self.ss = [k.sb("ss%d" % i, [P, 1], F32) for i in range(2)]
        self.rstd = [k.sb("rstd%d" % i, [P, 1], F32) for i in range(2)]
        self.tmp = [k.sb("ntmp%d" % i, [P, D], F32) for i in range(2)]
        self.hb = [k.sb("hb%d" % i, [P, D], BF16) for i in range(2)]
        self.i = 0


def norm_mod_T(k, cm, xap, xb, A, Bv, hT_ap, hTb, tbank):
    i = cm.i = (cm.i + 1) % 2
    ss, rstd, tmp, hb = cm.ss[i], cm.rstd[i], cm.tmp[i], cm.hb[i]
    k.act(cm.junk[:], xap, AF.Square, [xb], [cm.junk.b, ss.b], accum=ss[:])
    emit_rstd(k, ss, rstd, D, None)
    k.stt("dve", tmp[:], xap, rstd[:], A[:], ALU.mult, ALU.mult, [xb, rstd.b, A.b], [tmp.b])
    k.tt("pool", hb[:], tmp[:], Bv[:], ALU.add, [tmp.b, Bv.b], [hb.b])
    pb = tbank.t[:].bitcast(BF16)
    for f in range(8):
        k.tr(pb[:, f * P:(f + 1) * P], hb[:, f * P:(f + 1) * P], cm.ident[:],
             [hb.b, cm.ident.b], [tbank.b])
    k.copy("act", hT_ap, pb.rearrange("p (f t) -> p f t", f=8), [tbank.b], [hTb])


def resid_update(k, cm, xap, xb, y, G):
    i = cm.i = (cm.i + 1) % 2
    ss, rstd, tmp = cm.ss[i], cm.rstd[i], cm.tmp[i]
    k.act(cm.junk[:], y[:], AF.Square, [y.b], [cm.junk.b, ss.b], accum=ss[:])
    emit_rstd(k, ss, rstd, D, None)
    k.stt("dve", tmp[:], y[:], rstd[:], G[:], ALU.mult, ALU.mult, [y.b, rstd.b, G.b], [tmp.b])
    k.tt("pool", xap, xap, tmp[:], ALU.add, [xb, tmp.b], [xb])


def load_mod_vectors(k, mod, normg, layer, which, tmps=None):
    o = 0 if which == "a" else 3
    n0 = 0 if which == "a" else 2
    sh = bcast_load(k, "sp", "sh" + which, mod, mod[layer, (o + 0) * D:(o + 1) * D])
    sc = bcast_load(k, "sp", "sc" + which, mod, mod[layer, (o + 1) * D:(o + 2) * D])
    gt = bcast_load(k, "sp", "gt" + which, mod, mod[layer, (o + 2) * D:(o + 3) * D])
    if tmps is not None:
        g0, g1 = tmps
        for t_, r_ in ((g0, n0), (g1, n0 + 1)):
            k.dmasem(t_.b)
            k.load("sp", t_, t_[:], normg, normg[layer, r_].partition_broadcast(P))
    else:
        g0 = bcast_load(k, "sp", "g0" + which, normg, normg[layer, n0])
        g1 = bcast_load(k, "sp", "g1" + which, normg, normg[layer, n0 + 1])
    k.stt("dve", sc[:], sc[:], 1.0, g0[:], ALU.add, ALU.mult, [sc.b, g0.b], [sc.b])
    k.tt("dve", gt[:], gt[:], g1[:], ALU.mult, [gt.b, g1.b], [gt.b])
    return sc, sh, gt


def emit_ffn(k, cm, banks, xs, NT, w1, w2, A2, B2, G3, layer, xio=None):
    G = min(4, NT)
    hT = k.sb("f_hT", [P, 8, G * P], BF16)
    uT = k.sb("f_uT", [P, 32, G * P], BF16)
    uTb = [k.sub("f_uT%d" % c) for c in range(8)]
    w1c = [k.sb("f_w1c%d" % i, [P, 8, 512], BF16, dma=True) for i in range(2)]
    w2c = [k.sb("f_w2c%d" % i, [P, 4, 512], BF16, dma=True) for i in range(2)]
    rl = [k.sb("f_rl%d" % i, [P, G * P], F32) for i in range(2)]
    yt = [k.sb("f_y%d" % i, [P, D], F32) for i in range(G)]
    tbank, ub, yb = banks[0], banks[1:3], banks[3:7]
    N = G * P
    nw1 = nw2 = nr = 0
    for g in range(NT // G):
        for j in range(G):
            t = g * G + j
            if xio is not None:
                k.load("sp", xs[t], xs[t][:], xio[0], xio[0][t * P:(t + 1) * P, :])
            norm_mod_T(k, cm, xs[t][:], xs[t].b, A2, B2, hT[:, :, j * P:(j + 1) * P], hT.b, tbank)
        for c in range(8):
            wc = w1c[nw1 % 2]
            nw1 += 1
            for kt in range(8):
                k.load("pool", wc, wc[:, kt, :], w1,
                       w1[layer, kt * P:(kt + 1) * P, c * 512:(c + 1) * 512])
            for j in range(4):
                bk = ub[(c * 4 + j) % 2]
                for kt in range(8):
                    k.mm(bk[:, 0:N], wc[:, kt, j * P:(j + 1) * P], hT[:, kt, :],
                         kt == 0, kt == 7, [wc.b, hT.b], [bk.b])
                r = rl[nr % 2]
                nr += 1
                k.act(r[:], bk[:, 0:N], AF.Relu, [bk.b], [r.b])
                k.tt("pool", uT[:, c * 4 + j, :], r[:], r[:], ALU.mult, [r.b], [uTb[c]])
        for half in range(2):
            for c in range(8):
                wc = w2c[nw2 % 2]
                nw2 += 1
                k.load("pool", wc, wc[:], w2,
                       w2[layer, c * 512:(c + 1) * 512, half * 512:(half + 1) * 512]
                       .rearrange("(k p) n -> p k n", p=P))
                for kk in range(4):
                    ff = c * 4 + kk
                    for j in range(G):
                        k.mm(yb[j][:, :], uT[:, ff, j * P:(j + 1) * P], wc[:, kk, :],
                             ff == 0, ff == 31, [wc.b, uTb[c]], [yb[j].b])
            for j in range(G):
                y = yt[j]
                if half == 0:
                    k.copy("act", y[:, 0:512], yb[j][:, :], [yb[j].b], [y.b])
                else:
                    k.copy("dve", y[:, 512:1024], yb[j][:, :], [yb[j].b], [y.b])
                    t = g * G + j
                    resid_update(k, cm, xs[t][:], xs[t].b, y, G3)
                    if xio is not None:
                        k.dma("sp", xio[1][t * P:(t + 1) * P, :], xs[t][:], [xs[t].b], [xio[1].b], xio[1].b)
    return


TWO_PI_HI = 6.28125
TWO_PI_LO = 0.0019353071795864769
PI = 3.14159265358979


def emit_sincos(k, name, posB, invf, nrow, T_):
    CH = min(512, T_)
    ang = k.sb(name + "_ang", [nrow, CH], F32)
    kf = k.sb(name + "_kf", [nrow, CH], F32)
    ki = k.sb(name + "_ki", [nrow, CH], I32)
    sn = k.sb(name + "_sin", [nrow, T_], F32)
    cs = k.sb(name + "_cos", [nrow, T_], F32)
    for c in range(T_ // CH):
        sl = slice(c * CH, (c + 1) * CH)
        k.ts("dve", ang[:], posB[:, sl], invf[:, 0:1], None, ALU.mult, None, [posB.b, invf.b], [ang.b])
        for dst, off in ((sn, 0.0), (cs, PI / 2)):
            r = dst[:, sl]
            rb = dst.b
            if off:
                k.ts("dve", r, ang[:], off, None, ALU.add, None, [ang.b], [rb])
                src, srcb = r, rb
            else:
                src, srcb = ang[:], ang.b
            k.ts("dve", kf[:], src, 1.0 / (2 * PI), 0.5, ALU.mult, ALU.add, [srcb], [kf.b])
            k.copy("dve", ki[:], kf[:], [kf.b], [ki.b])
            k.copy("dve", kf[:], ki[:], [ki.b], [kf.b])
            k.stt("dve", r, kf[:], -TWO_PI_HI, src, ALU.mult, ALU.add, [kf.b, srcb], [rb])
            k.stt("dve", r, kf[:], -TWO_PI_LO, r, ALU.mult, ALU.add, [kf.b, rb], [rb])
            k.ts("dve", kf[:], r, -PI, None, ALU.is_lt, None, [rb], [kf.b])
            k.stt("dve", r, kf[:], 2 * PI, r, ALU.mult, ALU.add, [kf.b, rb], [rb])
            k.ts("dve", kf[:], r, PI, None, ALU.is_gt, None, [rb], [kf.b])
            k.stt("dve", r, kf[:], -2 * PI, r, ALU.mult, ALU.add, [kf.b, rb], [rb])
            k.ts("dve", r, r, -PI, PI, ALU.max, ALU.min, [rb], [rb])
            k.act(r, r, AF.Sin, [rb], [rb])
    return cs, sn


def emit_rope_T(k, A, Ab, Bm, Bb, cosT, sinT, sl, out, outb, scale, tmp1, tmp2, nrow):
    k.stt("dve", tmp1[0:nrow, :], A, scale, cosT[0:nrow, sl], ALU.mult, ALU.mult,
          [Ab, cosT.b], [tmp1.b])
    k.stt("dve", tmp2[0:nrow, :], Bm, scale, sinT[0:nrow, sl], ALU.mult, ALU.mult,
          [Bb, sinT.b], [tmp2.b])
    k.tt("pool", out, tmp1[0:nrow, :], tmp2[0:nrow, :], ALU.add, [tmp1.b, tmp2.b], [outb])


def make_rot(k, dst, src_view_lo, src_view_hi, dst_view_lo, dst_view_hi, srcb):
    k.ts("dve", dst_view_lo, src_view_hi, -1.0, None, ALU.mult, None, [srcb], [dst.b])
    k.copy("dve", dst_view_hi, src_view_lo, [srcb], [dst.b])


MLA_SCALE = 192 ** -0.5


def stage_mla_pre(S, layer, j):
    T_ = S // NCORES
    NT = T_ // P
    k = K()
    x = k.dram("x", [T_, D], F32, "ExternalInput")
    pos = k.dram("pos", [T_], I32, "ExternalInput")
    mod = k.dram("mod", [4, 6 * D], F32, "ExternalInput")
    normg = k.dram("normg", [4, 4, D], F32, "ExternalInput")
    w_in = k.dram("w_in", [1, D, 448], F32, "ExternalInput")
    qng = k.dram("qng", [1, 256], F32, "ExternalInput")
    kvg = k.dram("kvg", [1, 128], F32, "ExternalInput")
    wqup = k.dram("wqup", [1, 256, 1536], F32, "ExternalInput")
    wkvup = k.dram("wkvup", [1, 128, 2048], F32, "ExternalInput")
    consts = k.dram("consts", [P, P], F32, "ExternalInput")
    invf_d = k.dram("invf64", [64, 1], F32, "ExternalInput")
    QT = k.dram("QT", [8, 192, T_], BF16, "ExternalOutput")
    KcT = k.dram("KcT", [P, T_], BF16, "ExternalOutput")
    KpT = k.dram("KpT", [64, T_], BF16, "ExternalOutput")
    Vc = k.dram("Vc", [T_, P], BF16, "ExternalOutput")
    banks = k.psum_banks()
    cm = Common(k, consts)
    A0, B0, _ = load_mod_vectors(k, mod, normg, layer, "a")
    win = k.sb("win", [P, 8, 448], BF16, dma=True)
    k.load("pool", win, win[:], w_in, w_in[j].rearrange("(k p) n -> p k n", p=P))
    wq = k.sb("wq", [P, 2, 1536], BF16, dma=True)
    k.load("pool", wq, wq[:], wqup, wqup[j].rearrange("(k p) n -> p k n", p=P))
    wkv = k.sb("wkv", [P, 2048], BF16, dma=True)
    k.load("pool", wkv, wkv[:], wkvup, wkvup[j])
    qg = bcast_load(k, "sp", "qg", qng, qng[j])
    kg = bcast_load(k, "sp", "kg", kvg, kvg[j])
    invf = k.sb("invf", [64, 1], F32, dma=True)
    k.load("sp", invf, invf[:], invf_d, invf_d[:, :])
    posB = k.sb("posB", [64, T_], F32, dma=True)
    k.load("pool", posB, posB[:], pos, pos.t.partition_broadcast(64))
    cosT, sinT = emit_sincos(k, "rp", posB, invf, 64, T_)
    winr = k.sb("winr", [P, 8, 64], BF16)
    make_rot(k, winr, win[:, :, 384:416], win[:, :, 416:448], winr[:, :, 0:32], winr[:, :, 32:64], win.b)
    wqr = k.sb("wqr", [P, 2, 8, 64], BF16)
    for r in range(2):
        v = wq[:, r, :].rearrange("p (h c) -> p h c", c=192)
        make_rot(k, wqr, v[:, :, 128:160], v[:, :, 160:192], wqr[:, r, :, 0:32], wqr[:, r, :, 32:64], wq.b)
    tb = banks[0]
    pb = tb.t[:].bitcast(BF16)
    WukT = k.sb("WukT", [P, 8, P], BF16)
    for h in range(8):
        k.tr(pb[:, h * P:(h + 1) * P], wkv[:, h * 256:h * 256 + 128], cm.ident[:], [wkv.b, cm.ident.b], [tb.b])
    k.copy("dve", WukT[:], pb.rearrange("p (f t) -> p f t", f=8), [tb.b], [WukT.b])

    hT = k.sb("hT", [P, 8, T_], BF16)
    qnT = k.sb("qnT", [P, 2, T_], BF16)
    kcT = k.sb("kcT", [P, T_], BF16)
    xt2 = [k.sb("xt%d" % i, [P, D], F32, dma=True) for i in range(2)]
    qn_tok = [k.sb("qn_tok%d" % i, [P, 256], BF16) for i in range(2)]
    cn_tok = [k.sb("cn_tok%d" % i, [P, P], BF16) for i in range(2)]
    st = [[k.sb("st%d_%d" % (i, q), [P, 1], F32) for q in range(4)] for i in range(2)]
    lb = banks[1]
    for t in range(NT):
        i = t % 2
        sl = slice(t * P, (t + 1) * P)
        xt = xt2[i]
        k.load("sp", xt, xt[:], x, x[sl, :])
        norm_mod_T(k, cm, xt[:], xt.b, A0, B0, hT[:, :, sl], hT.b, banks[0])
        for kt in range(8):
            k.mm(lb[:, 0:384], hT[:, kt, sl], win[:, kt, 0:384], kt == 0, kt == 7, [hT.b, win.b], [lb.b])
        ssq, rq, ssk, rk = st[i]
        k.act(cm.junk[:, 0:256], lb[:, 0:256], AF.Square, [lb.b], [cm.junk.b, ssq.b], accum=ssq[:])
        k.act(cm.junk[:, 256:384], lb[:, 256:384], AF.Square, [lb.b], [cm.junk.b, ssk.b], accum=ssk[:])
        emit_rstd(k, ssq, rq, 256, None)
        emit_rstd(k, ssk, rk, 128, None)
        k.stt("dve", qn_tok[i][:], lb[:, 0:256], rq[:], qg[:], ALU.mult, ALU.mult, [lb.b, rq.b, qg.b], [qn_tok[i].b])
        k.stt("dve", cn_tok[i][:], lb[:, 256:384], rk[:], kg[:], ALU.mult, ALU.mult, [lb.b, rk.b, kg.b], [cn_tok[i].b])
        k.dma("sp", Vc[sl, :], cn_tok[i][:], [cn_tok[i].b], [Vc.b], Vc.b)
        for f in range(2):
            k.tr(pb[:, f * P:(f + 1) * P], qn_tok[i][:, f * P:(f + 1) * P], cm.ident[:], [qn_tok[i].b, cm.ident.b], [tb.b])
        k.tr(pb[:, 2 * P:3 * P], cn_tok[i][:], cm.ident[:], [cn_tok[i].b, cm.ident.b], [tb.b])
        k.copy("act", qnT[:, :, sl], pb[:, 0:2 * P].rearrange("p (f t) -> p f t", f=2), [tb.b], [qnT.b])
        k.copy("act", kcT[:, sl], pb[:, 2 * P:3 * P], [tb.b], [kcT.b])
    k.dma("sp", KcT[:, :], kcT[:], [kcT.b], [KcT.b], KcT.b)
    NC = min(512, T_)
    tmp1 = k.sb("rt1", [P, NC], F32)
    tmp2 = k.sb("rt2", [P, NC], F32)
    qnope = [k.sb("qnope%d" % i, [P, NC], BF16) for i in range(2)]
    qo = [k.sb("qo%d" % i, [P, NC], BF16) for i in range(2)]
    po = [k.sb("po%d" % i, [64, NC], BF16) for i in range(2)]
    n = 0
    for c in range(T_ // NC):
        sl = slice(c * NC, (c + 1) * NC)
        bA, bB = banks[2], banks[3]
        for kt in range(8):
            k.mm(bA[0:64, 0:NC], win[:, kt, 384:448], hT[:, kt, sl], kt == 0, kt == 7, [win.b, hT.b], [bA.b])
        for kt in range(8):
            k.mm(bB[0:64, 0:NC], winr[:, kt, :], hT[:, kt, sl], kt == 0, kt == 7, [winr.b, hT.b], [bB.b])
        pp = po[n % 2]
        emit_rope_T(k, bA[0:64, 0:NC], bA.b, bB[0:64, 0:NC], bB.b, cosT, sinT, sl, pp[:], pp.b, 1.0, tmp1, tmp2, 64)
        k.dma("sp", KpT[:, sl], pp[:], [pp.b], [KpT.b], KpT.b)
        n += 1
        for h in range(8):
            bN, bQ = banks[4 + (h % 2)], banks[6 + (h % 2)]
            for r in range(2):
                k.mm(bN[:, 0:NC], wq[:, r, h * 192:h * 192 + 128], qnT[:, r, sl], r == 0, r == 1, [wq.b, qnT.b], [bN.b])
            qq = qnope[h % 2]
            k.copy("act", qq[:], bN[:, 0:NC], [bN.b], [qq.b])
            k.mm(bQ[:, 0:NC], WukT[:, h, :], qq[:], True, True, [WukT.b, qq.b], [bQ.b])
            q2 = qo[h % 2]
            k.act(q2[:], bQ[:, 0:NC], AF.Copy, [bQ.b], [q2.b], scale=MLA_SCALE)
            k.dma("sp", QT[h, 0:128, sl], q2[:], [q2.b], [QT.b], QT.b)
            for r in range(2):
                k.mm(bA[0:64, 0:NC], wq[:, r, h * 192 + 128:h * 192 + 192], qnT[:, r, sl], r == 0, r == 1, [wq.b, qnT.b], [bA.b])
            for r in range(2):
                k.mm(bB[0:64, 0:NC], wqr[:, r, h, :], qnT[:, r, sl], r == 0, r == 1, [wqr.b, qnT.b], [bB.b])
            pp = po[n % 2]
            n += 1
            emit_rope_T(k, bA[0:64, 0:NC], bA.b, bB[0:64, 0:NC], bB.b, cosT, sinT, sl, pp[:], pp.b, MLA_SCALE, tmp1, tmp2, 64)
            k.dma("sp", QT[h, 128:192, sl], pp[:], [pp.b], [QT.b], QT.b)
    return k


def attn_tail(k, cm, banks, oT, wout, xt, G1, xo, sl, yt, nh, oaps=None):
    if oaps is None:
        oaps = [oT[:, h * P:(h + 1) * P] for h in range(nh)]
    for half in range(2):
        yb = banks[3 + half]
        for h in range(nh):
            k.mm(yb[:, :], oaps[h], wout[:, h, half * 512:(half + 1) * 512],
                 h == 0, h == nh - 1, [oT.b, wout.b], [yb.b])
        k.copy("act" if half == 0 else "dve", yt[:, half * 512:(half + 1) * 512], yb[:, :], [yb.b], [yt.b])
    resid_update(k, cm, xt[:], xt.b, yt, G1)
    k.dma("sp", xo[sl, :], xt[:], [xt.b], [xo.b], xo.b)


def stage_mla_attn(S, layer, j):
    T_ = S // NCORES
    NT = T_ // P
    NB = S // P
    k = K()
    x = k.dram("x", [T_, D], F32, "ExternalInput")
    QT = k.dram("QT", [8, 192, T_], BF16, "ExternalInput")
    KcTg = k.dram("KcTg", [P, S], BF16, "ExternalInput")
    KpTg = k.dram("KpTg", [64, S], BF16, "ExternalInput")
    Vcg = k.dram("Vcg", [S, P], BF16, "ExternalInput")
    mask = k.dram("mask", [P, 8, 512], F32, "ExternalInput")
    mod = k.dram("mod", [4, 6 * D], F32, "ExternalInput")
    normg = k.dram("normg", [4, 4, D], F32, "ExternalInput")
    wkvup = k.dram("wkvup", [1, 128, 2048], F32, "ExternalInput")
    w_out = k.dram("w_out", [1, D, D], F32, "ExternalInput")
    consts = k.dram("consts", [P, P], F32, "ExternalInput")
    xo = k.dram("xo", [T_, D], F32, "ExternalOutput")
    banks = k.psum_banks()
    cm = Common(k, consts)
    _, _, G1 = load_mod_vectors(k, mod, normg, layer, "a")
    KcT = k.sb("KcT", [P, S], BF16, dma=True)
    KpT = k.sb("KpT", [64, S], BF16, dma=True)
    V = k.sb("V", [P, NB, 129], BF16, dma=True)
    k.load("sp", KcT, KcT[:], KcTg, KcTg[:, :])
    k.load("sp", KpT, KpT[:], KpTg, KpTg[:, :])
    k.memset("pool", V[:, :, 128:129], 1.0, [V.b])
    vsrc = Vcg.t.rearrange("(b s) l -> s b l", s=P)
    for c in range(0, NB, 16):
        k.load("sp", V, V[:, c:c + 16, 0:128], Vcg, vsrc[:, c:c + 16, :])
    mk = k.sb("mk", [P, 8, 512], BF16, dma=True)
    k.load("pool", mk, mk[:], mask, mask[:, :, :])
    wkv = k.sb("wkv", [P, 2048], BF16, dma=True)
    k.load("pool", wkv, wkv[:], wkvup, wkvup[j])
    wout = k.sb("wout", [P, 8, D], BF16, dma=True)
    k.load("pool", wout, wout[:], w_out, w_out[j].rearrange("(k p) n -> p k n", p=P))
    QTq = [k.sb("QTq%d" % i, [P, 8, P], BF16, dma=True) for i in range(2)]
    QTp = [k.sb("QTp%d" % i, [64, 8, P], BF16, dma=True) for i in range(2)]
    xts = [k.sb("xt%d" % i, [P, D], F32, dma=True) for i in range(2)]
    PT = [k.sb("PT%d" % i, [P, 1024], BF16) for i in range(2)]
    rin = k.sb("rin", [P, 8], F32)
    oc = k.sb("oc", [P, 1024], BF16)
    ocT = k.sb("ocT", [P, 1024], BF16)
    oT = k.sb("oT", [P, 1024], BF16)
    yt = k.sb("yt", [P, D], F32)
    tb = banks[0]
    pb = tb.t[:].bitcast(BF16)
    accb = banks[5:8]

    def acc(h):
        return accb[h // 3], (h % 3) * 129

    for jq in range(NT):
        i = jq % 2
        sl = slice(jq * P, (jq + 1) * P)
        k.load("sp", QTq[i], QTq[i][:], QT, QT[:, 0:128, sl].rearrange("h l t -> l h t"))
        k.load("sp", QTp[i], QTp[i][:], QT, QT[:, 128:192, sl].rearrange("h l t -> l h t"))
        xt = xts[i]
        k.load("sp", xt, xt[:], x, x[sl, :])
        qq = QTq[i][:].rearrange("p h t -> p (h t)")
        qp = QTp[i][:].rearrange("p h t -> p (h t)")
        for b in accb:
            k.memset("dve", b[:, :], 0.0, [b.b])
        nkb = 8 * (jq + 1)
        for kb in range(nkb):
            ks = slice(kb * P, (kb + 1) * P)
            pt = PT[kb % 2]
            for half in range(2):
                bS = banks[1 + 2 * (kb % 2) + half]
                hs = slice(half * 512, (half + 1) * 512)
                k.mm(bS[:, :], KcT[:, ks], qq[:, hs], True, False, [KcT.b, QTq[i].b], [bS.b])
                k.mm(bS[:, :], KpT[:, ks], qp[:, hs], False, True, [KpT.b, QTp[i].b], [bS.b])
                k.act(pt[:, hs], bS[:, :], AF.Exp, [bS.b], [pt.b])
                if kb >= nkb - 8:
                    g = kb - (nkb - 8)
                    k.tt("pool", pt[:, hs], pt[:, hs], mk[:, g, :], ALU.mult, [pt.b, mk.b], [pt.b])
            for h in range(8):
                b, c0 = acc(h)
                k.mm(b[:, c0:c0 + 129], pt[:, h * P:(h + 1) * P], V[:, kb, :], False, False,
                     [pt.b, V.b], [b.b], skip=True)
        for h in range(8):
            b, c0 = acc(h)
            k.recip(rin[:, h:h + 1], b[:, c0 + 128:c0 + 129], [b.b], [rin.b])
            k.ts("dve", oc[:, h * P:(h + 1) * P], b[:, c0:c0 + 128], rin[:, h:h + 1], None, ALU.mult, None,
                 [b.b, rin.b], [oc.b])
        for h in range(8):
            k.tr(pb[:, h * P:(h + 1) * P], oc[:, h * P:(h + 1) * P], cm.ident[:], [oc.b, cm.ident.b], [tb.b])
        k.copy("act", ocT[:], pb, [tb.b], [ocT.b])
        for h in range(8):
            ob = banks[1 + h // 4]
            c0 = (h % 4) * P
            k.mm(ob[:, c0:c0 + P], wkv[:, h * 256 + 128:h * 256 + 256], ocT[:, h * P:(h + 1) * P], True, True,
                 [wkv.b, ocT.b], [ob.b], skip=True)
        k.copy("act", oT[:, 0:512], banks[1][:, :], [banks[1].b], [oT.b])
        k.copy("dve", oT[:, 512:1024], banks[2][:, :], [banks[2].b], [oT.b])
        attn_tail(k, cm, banks, oT, wout, xt, G1, xo, sl, yt, 8)
    return k


def mla_masks(core, strict=False, rep=4):
    m = np.zeros((P, 8, rep, P), np.float32)
    s_ = np.arange(P)[:, None]
    t_ = np.arange(P)[None, :]
    tri = (s_ < t_) if strict else (s_ <= t_)
    for g in range(8):
        if g < core:
            m[:, g] = 1.0
        elif g == core:
            m[:, g] = tri.astype(np.float32)[:, None, :]
    return m.reshape(P, 8, rep * P)


def stage_sb_pre(S, layer):
    T_ = S // NCORES
    NT = T_ // P
    k = K()
    x = k.dram("x", [T_, D], F32, "ExternalInput")
    mod = k.dram("mod", [4, 6 * D], F32, "ExternalInput")
    normg = k.dram("normg", [4, 4, D], F32, "ExternalInput")
    w_in = k.dram("w_in", [1, D, 3072], F32, "ExternalInput")
    consts = k.dram("consts", [P, P], F32, "ExternalInput")
    QT = k.dram("QT", [D, T_], BF16, "ExternalOutput")
    KT = k.dram("KT", [D, T_], BF16, "ExternalOutput")
    V = k.dram("V", [T_, D], BF16, "ExternalOutput")
    banks = k.psum_banks()
    cm = Common(k, consts)
    A0, B0, _ = load_mod_vectors(k, mod, normg, layer, "a")
    win = k.sb("win", [P, 8, 3072], BF16, dma=True)
    for kt in range(8):
        k.load("pool", win, win[:, kt, :], w_in, w_in[0, kt * P:(kt + 1) * P, :])
    hT = k.sb("hT", [P, 8, T_], BF16)
    xt2 = [k.sb("xt%d" % i, [P, D], F32, dma=True) for i in range(2)]
    vt = [k.sb("vt%d" % i, [P, D], BF16) for i in range(2)]
    for t in range(NT):
        i = t % 2
        sl = slice(t * P, (t + 1) * P)
        k.load("sp", xt2[i], xt2[i][:], x, x[sl, :])
        norm_mod_T(k, cm, xt2[i][:], xt2[i].b, A0, B0, hT[:, :, sl], hT.b, banks[0])
        for half in range(2):
            vb = banks[1 + half]
            for kt in range(8):
                k.mm(vb[:, :], hT[:, kt, sl], win[:, kt, 2048 + half * 512:2048 + (half + 1) * 512],
                     kt == 0, kt == 7, [hT.b, win.b], [vb.b])
            k.copy("act" if half == 0 else "dve", vt[i][:, half * 512:(half + 1) * 512], vb[:, :], [vb.b], [vt[i].b])
        k.dma("sp", V[sl, :], vt[i][:], [vt[i].b], [V.b], V.b)
    NC = min(512, T_)
    ot = [k.sb("ot%d" % i, [P, NC], BF16) for i in range(2)]
    n = 0
    for c in range(T_ // NC):
        sl = slice(c * NC, (c + 1) * NC)
        for m in range(16):
            bk = banks[3 + (m % 2)]
            for kt in range(8):
                k.mm(bk[:, 0:NC], win[:, kt, m * P:(m + 1) * P], hT[:, kt, sl], kt == 0, kt == 7, [win.b, hT.b], [bk.b])
            o = ot[n % 2]
            n += 1
            k.act(o[:], bk[:, 0:NC], AF.Copy, [bk.b], [o.b], scale=0.125 if m < 8 else 1.0)
            dst = QT if m < 8 else KT
            mm_ = m % 8
            k.dma("sp", dst[mm_ * P:(mm_ + 1) * P, sl], o[:], [o.b], [dst.b], dst.b)
    return k


def load_gate(k, mod, normg, layer, which):
    o = 2 if which == "a" else 5
    n1 = 1 if which == "a" else 3
    gt = bcast_load(k, "sp", "gt" + which, mod, mod[layer, o * D:(o + 1) * D])
    g1 = bcast_load(k, "sp", "g1" + which, normg, normg[layer, n1])
    k.tt("dve", gt[:], gt[:], g1[:], ALU.mult, [gt.b, g1.b], [gt.b])
    return gt


def stage_sb_attn(S, layer):
    T_ = S // NCORES
    NT = T_ // P
    NB = S // P
    k = K()
    x = k.dram("x", [T_, D], F32, "ExternalInput")
    QT = k.dram("QT", [D, T_], BF16, "ExternalInput")
    KTg = k.dram("KTg", [D, S], BF16, "ExternalInput")
    Vg = k.dram("Vg", [S, D], BF16, "ExternalInput")
    mask = k.dram("mask", [P, 8, P], F32, "ExternalInput")
    ntri_d = k.dram("ntri", [P, P], F32, "ExternalInput")
    mod = k.dram("mod", [4, 6 * D], F32, "ExternalInput")
    normg = k.dram("normg", [4, 4, D], F32, "ExternalInput")
    w_out = k.dram("w_out", [1, D, D], F32, "ExternalInput")
    consts = k.dram("consts", [P, P], F32, "ExternalInput")
    xo = k.dram("xo", [T_, D], F32, "ExternalOutput")
    banks = k.psum_banks()
    cm = Common(k, consts)
    G1 = load_gate(k, mod, normg, layer, "a")
    mk = k.sb("mk", [P, 8, P], BF16, dma=True)
    k.load("pool", mk, mk[:], mask, mask[:, :, :])
    ntri = k.sb("ntri_s", [P, P], BF16, dma=True)
    k.load("pool", ntri, ntri[:], ntri_d, ntri_d[:, :])
    wout = k.sb("wout", [P, 8, D], BF16, dma=True)
    k.load("pool", wout, wout[:], w_out, w_out[0].rearrange("(k p) n -> p k n", p=P))
    onef = k.sb("onef", [P, 1], F32)
    k.memset("dve", onef[:], 1.0, [onef.b])
    oneb = k.sb("oneb", [P, 1], BF16)
    k.memset("dve", oneb[:], 1.0, [oneb.b])
    KT = k.sb("KT", [64, S], BF16, dma=True)
    V = k.sb("V", [P, NB, 64], BF16, dma=True)
    QTs = k.sb("QTs", [64, T_], BF16, dma=True)
    oT_all = k.sb("oT_all", [P, 8, T_], BF16)
    Opair = k.sb("Opair", [P, NT, P], BF16)
    E = [k.sb("E%d" % i, [P, 512], F32) for i in range(2)]
    L = [k.sb("L%d" % i, [P, 512], BF16) for i in range(2)]
    PT = [k.sb("PT%d" % i, [P, 512], BF16) for i in range(2)]
    fsc = [k.sb("fsc%d" % i, [P, 4], F32) for i in range(2)]
    Oacc = k.sb("Oacc", [P, 64], F32)
    tb = banks[0]
    pb = tb.t[:].bitcast(BF16)
    vsrc = Vg.t.rearrange("(b s) c -> s b c", s=P)
    u = 0
    for hd in range(16):
        hp, a = hd // 2, hd % 2
        k.load("sp", KT, KT[:], KTg, KTg[hd * 64:(hd + 1) * 64, :])
        for c in range(0, NB, 16):
            k.load("sp", V, V[:, c:c + 16, :], Vg, vsrc[:, c:c + 16, hd * 64:(hd + 1) * 64])
        k.load("sp", QTs, QTs[:], QT, QT[hd * 64:(hd + 1) * 64, :])
        for jq in range(NT):
            sl = slice(jq * P, (jq + 1) * P)
            k.memset("dve", Oacc[:], 0.0, [Oacc.b])
            nkb = 8 * (jq + 1)
            for kq in range(nkb // 4):
                i = u % 2
                u += 1
                bZ, bO = banks[1 + i], banks[3 + i]
                e_, l_, p_, f_ = E[i], L[i], PT[i], fsc[i]
                for w in range(4):
                    kb = 4 * kq + w
                    k.mm(bZ[:, w * P:(w + 1) * P], KT[:, kb * P:(kb + 1) * P], QTs[:, sl],
                         w == 0, False, [KT.b, QTs.b], [bZ.b], skip=True)
                k.act(e_[:], bZ[:, :], AF.Exp, [bZ.b], [e_.b])
                k.act(l_[:], e_[:], AF.Ln, [e_.b, onef.b], [l_.b], bias=onef[:, 0:1])
                masked = 4 * kq >= nkb - 8
                if masked:
                    g0 = 4 * kq - (nkb - 8)
                    mv = mk[:, g0:g0 + 4, :].rearrange("p g c -> p (g c)")
                    k.tt("pool", l_[:], l_[:], mv, ALU.mult, [l_.b, mk.b], [l_.b])
                k.mm(bZ[:, :], ntri[:], l_[:], False, True, [ntri.b, l_.b], [bZ.b], skip=True)
                k.act(p_[:], bZ[:, :], AF.Exp, [bZ.b], [p_.b])
                if masked:
                    k.tt("pool", p_[:], p_[:], mv, ALU.mult, [p_.b, mk.b], [p_.b])
                for w in range(4):
                    kb = 4 * kq + w
                    k.mm(bO[:, w * 64:(w + 1) * 64], p_[:, w * P:(w + 1) * P], V[:, kb, :],
                         w == 0, False, [p_.b, V.b], [bO.b], skip=True)
                for w in range(4):
                    k.mm(bO[:, 256 + w:257 + w], l_[:, w * P:(w + 1) * P], oneb[:], False, w == 3,
                         [l_.b, oneb.b], [bO.b], skip=True)
                k.act(f_[:], bO[:, 256:260], AF.Exp, [bO.b], [f_.b], scale=-1.0)
                for w in range(4):
                    k.stt("dve", Oacc[:], Oacc[:], f_[:, w:w + 1], bO[:, w * 64:(w + 1) * 64],
                          ALU.mult, ALU.add, [Oacc.b, f_.b, bO.b], [Oacc.b])
            k.copy("act", Opair[:, jq, a * 64:(a + 1) * 64], Oacc[:], [Oacc.b], [Opair.b])
            if a == 1:
                k.tr(pb[:, 0:P], Opair[:, jq, :], cm.ident[:], [Opair.b, cm.ident.b], [tb.b])
                k.copy("act", oT_all[:, hp, sl], pb[:, 0:P], [tb.b], [oT_all.b])
    xts = [k.sb("xt%d" % i, [P, D], F32, dma=True) for i in range(2)]
    yt = k.sb("yt", [P, D], F32)
    for jq in range(NT):
        sl = slice(jq * P, (jq + 1) * P)
        xt = xts[jq % 2]
        k.load("sp", xt, xt[:], x, x[sl, :])
        attn_tail(k, cm, banks, oT_all, wout, xt, G1, xo, sl, yt, 8, oaps=[oT_all[:, h, sl] for h in range(8)])
    return k


RET_G = [1.0 - 2.0 ** (-5.0 - h) for h in range(4)]


def ret_consts(core, NT):
    lg = np.log(np.asarray(RET_G, np.float64))
    idx = np.arange(P, dtype=np.float64)
    diff = idx[None, :] - idx[:, None]
    decayT = np.where(diff[None] >= 0, np.exp(np.maximum(diff, 0)[None] * lg[:, None, None]), 0.0)
    zeta = np.exp((P - 1.0 - idx)[:, None] * lg[None, :])
    xi = np.exp((idx[None, :] + 1.0) * lg[:, None])
    coef = np.zeros((P, 8, 4))
    for c2 in range(8):
        if c2 < core:
            coef[:, c2, :] = np.exp(P * NT * (core - 1 - c2) * lg)[None, :]
    return dict(decayT=np.ascontiguousarray(decayT.transpose(1, 0, 2)).astype(np.float32),
                zeta=zeta.astype(np.float32), xi=xi.reshape(1, 512).astype(np.float32),
                coef=coef.reshape(P, 32).astype(np.float32))


def stage_ret(S, layer, full):
    T_ = S // NCORES
    NT = T_ // P
    k = K()
    x = k.dram("x", [T_, D], F32, "ExternalInput")
    pos = k.dram("pos", [T_], I32, "ExternalInput")
    mod = k.dram("mod", [4, 6 * D], F32, "ExternalInput")
    normg = k.dram("normg", [4, 4, D], F32, "ExternalInput")
    w_in = k.dram("w_in", [1, D, 6144], F32, "ExternalInput")
    consts = k.dram("consts", [P, P], F32, "ExternalInput")
    invf_d = k.dram("invf128", [P, 1], F32, "ExternalInput")
    zeta_d = k.dram("zeta", [P, 4], F32, "ExternalInput")
    if full:
        gng = k.dram("gng", [1, 2048], F32, "ExternalInput")
        w_out = k.dram("w_out", [1, 2048, D], F32, "ExternalInput")
        decay_d = k.dram("decayT", [P, 4, P], F32, "ExternalInput")
        xi_d = k.dram("xi", [1, 512], F32, "ExternalInput")
        coef_d = k.dram("coef", [P, 32], F32, "ExternalInput")
        Eall = k.dram("Eall", [8, 4, 256, 512], F32, "ExternalInput")
        xo = k.dram("xo", [T_, D], F32, "ExternalOutput")
        Oscr = k.dram("Oscr", [T_, 2048], BF16, "Internal")
    else:
        Eo = k.dram("E", [4, 256, 512], F32, "ExternalOutput")
    banks = k.psum_banks()
    cm = Common(k, consts)
    A0, B0, G1 = load_mod_vectors(k, mod, normg, layer, "a", tmps=cm.tmp)
    invf = k.sb("invf", [P, 1], F32, dma=True)
    k.load("sp", invf, invf[:], invf_d, invf_d[:, :])
    zeta = k.sb("zeta", [P, 4], F32, dma=True)
    k.load("sp", zeta, zeta[:], zeta_d, zeta_d[:, :])
    posB = k.sb("posB", [P, T_], F32, dma=True)
    k.load("pool", posB, posB[:], pos, pos.t.partition_broadcast(P))
    cosT, sinT = emit_sincos(k, "rp", posB, invf, P, T_)
    if full:
        decay = k.sb("decay", [P, 4, P], BF16, dma=True)
        k.load("pool", decay, decay[:], decay_d, decay_d[:, :, :])
        xiB = k.sb("xiB", [P, 512], F32, dma=True)
        k.load("sp", xiB, xiB[:], xi_d, xi_d[0].partition_broadcast(P))
        coef = k.sb("coef", [P, 32], F32, dma=True)
        k.load("sp", coef, coef[:], coef_d, coef_d[:, :])
        gg = k.sb("gg", [P, 512], F32, dma=True)
    hT = k.sb("hT", [P, 8, T_], BF16)
    xt2 = [k.sb("xt%d" % i, [P, D], F32, dma=True) for i in range(2)]
    for t in range(NT):
        i = t % 2
        sl = slice(t * P, (t + 1) * P)
        k.load("sp", xt2[i], xt2[i][:], x, x[sl, :])
        norm_mod_T(k, cm, xt2[i][:], xt2[i].b, A0, B0, hT[:, :, sl], hT.b, banks[0])
    wq = k.sb("wq", [P, 8, 256], BF16, dma=True)
    wk = k.sb("wk", [P, 8, 256], BF16, dma=True)
    wv = k.sb("wv", [P, 8, 512], BF16, dma=True)
    qT = k.sb("qT", [P, 2, T_], BF16)
    kT = k.sb("kT", [P, 2, T_], BF16)
    kz = k.sb("kz", [P, NT, 256], BF16)
    Vt = k.sb("Vt", [P, NT, 512], BF16)
    state = k.sb("state", [P, 2, 512], F32, dma=True)
    stbf = k.sb("stbf", [P, 2, 512], BF16)
    NC = min(512, T_)
    m1 = k.sb("m1", [P, NC], F32)
    m2 = k.sb("m2", [P, NC], F32)
    if full:
        wg = k.sb("wg", [P, 8, 512], BF16, dma=True)
        qxT = k.sb("qxT", [P, 2, P], BF16)
        scm = k.sb("scm", [P, P], BF16)
        og = k.sb("og", [P, 512], F32)
        sg = k.sb("sg", [P, 512], F32)
        ob16 = [k.sb("ob16_%d" % i, [P, 512], BF16) for i in range(2)]
        Et = k.sb("Et", [P, 2, 512], F32, dma=True)
        gss = k.sb("gss", [P, 1], F32)
        grs = k.sb("grs", [P, 1], F32)
    tb = banks[0]
    pb = tb.t[:].bitcast(BF16)
    wsrc = w_in.t[0].rearrange("(k p) n -> p k n", p=P)
    nob = 0
    for h in range(4):
        k.load("pool", wq, wq[:], w_in, wsrc[:, :, h * 256:(h + 1) * 256])
        k.load("pool", wk, wk[:], w_in, wsrc[:, :, 1024 + h * 256:1024 + (h + 1) * 256])
        k.load("pool", wv, wv[:], w_in, wsrc[:, :, 2048 + h * 512:2048 + (h + 1) * 512])
        if full:
            k.load("pool", wg, wg[:], w_in, wsrc[:, :, 4096 + h * 512:4096 + (h + 1) * 512])
            k.load("sp", gg, gg[:], gng, gng[0, h * 512:(h + 1) * 512].partition_broadcast(P))
        for c in range(T_ // NC):
            sl = slice(c * NC, (c + 1) * NC)
            for (w_, dst, scl) in ((wq, qT, 1.0), (wk, kT, 1.0 / 16)) if full else ((wk, kT, 1.0 / 16),):
                bA, bB = banks[1], banks[2]
                for kt in range(8):
                    k.mm(bA[:, 0:NC], w_[:, kt, 0:128], hT[:, kt, sl], kt == 0, kt == 7, [w_.b, hT.b], [bA.b])
                for kt in range(8):
                    k.mm(bB[:, 0:NC], w_[:, kt, 128:256], hT[:, kt, sl], kt == 0, kt == 7, [w_.b, hT.b], [bB.b])
                k.stt("dve", m1[:], bA[:, 0:NC], scl, cosT[:, sl], ALU.mult, ALU.mult, [bA.b, cosT.b], [m1.b])
                k.stt("dve", m2[:], bB[:, 0:NC], scl, sinT[:, sl], ALU.mult, ALU.mult, [bB.b, sinT.b], [m2.b])
                k.tt("pool", dst[:, 0, sl], m1[:], m2[:], ALU.subtract, [m1.b, m2.b], [dst.b])
                k.stt("dve", m1[:], bB[:, 0:NC], scl, cosT[:, sl], ALU.mult, ALU.mult, [bB.b, cosT.b], [m1.b])
                k.stt("dve", m2[:], bA[:, 0:NC], scl, sinT[:, sl], ALU.mult, ALU.mult, [bA.b, sinT.b], [m2.b])
                k.tt("pool", dst[:, 1, sl], m1[:], m2[:], ALU.add, [m1.b, m2.b], [dst.b])
        for n in range(NT):
            sl = slice(n * P, (n + 1) * P)
            for d in range(2):
                k.tr(pb[:, d * P:(d + 1) * P], kT[:, d, sl], cm.ident[:], [kT.b, cm.ident.b], [tb.b])
            k.ts("dve", kz[:, n, :], pb[:, 0:256], zeta[:, h:h + 1], None, ALU.mult, None, [tb.b, zeta.b], [kz.b])
            vb = banks[3]
            for kt in range(8):
                k.mm(vb[:, :], hT[:, kt, sl], wv[:, kt, :], kt == 0, kt == 7, [hT.b, wv.b], [vb.b])
            k.copy("act", Vt[:, n, :], vb[:, :], [vb.b], [Vt.b])
        k.memset("dve", state[:], 0.0, [state.b])
        if full:
            for c2 in range(8):
                k.load("sp", Et, Et[:], Eall, Eall[c2, h].rearrange("(d p) v -> p d v", p=P))
                k.stt("dve", state[:], Et[:], coef[:, c2 * 4 + h:c2 * 4 + h + 1], state[:], ALU.mult, ALU.add,
                      [Et.b, coef.b, state.b], [state.b])
        k.copy("act", stbf[:], state[:], [state.b], [stbf.b])
        for n in range(NT):
            sl = slice(n * P, (n + 1) * P)
            if full:
                sb_, ob_, gb_ = banks[4], banks[5], banks[6]
                for d in range(2):
                    k.mm(sb_[:, 0:P], kT[:, d, sl], qT[:, d, sl], d == 0, d == 1, [kT.b, qT.b], [sb_.b])
                k.tt("dve", scm[:], sb_[:, 0:P], decay[:, h, :], ALU.mult, [sb_.b, decay.b], [scm.b])
                for d in range(2):
                    k.tt("pool", qxT[:, d, :], qT[:, d, sl], xiB[:, h * P:(h + 1) * P], ALU.mult,
                         [qT.b, xiB.b], [qxT.b])
                k.mm(ob_[:, :], scm[:], Vt[:, n, :], True, False, [scm.b, Vt.b], [ob_.b])
                for d in range(2):
                    k.mm(ob_[:, :], qxT[:, d, :], stbf[:, d, :], False, d == 1, [qxT.b, stbf.b], [ob_.b])
                for kt in range(8):
                    k.mm(gb_[:, :], hT[:, kt, sl], wg[:, kt, :], kt == 0, kt == 7, [hT.b, wg.b], [gb_.b])
                k.act(cm.junk[:, 0:512], ob_[:, :], AF.Square, [ob_.b], [cm.junk.b, gss.b], accum=gss[:])
                emit_rstd(k, gss, grs, 512, None)
                k.stt("dve", og[:], ob_[:, :], grs[:], gg[:], ALU.mult, ALU.mult,
                      [ob_.b, grs.b, gg.b], [og.b])
                k.act(sg[:], gb_[:, :], AF.Silu, [gb_.b], [sg.b])
                o16 = ob16[nob % 2]
                nob += 1
                k.tt("pool", o16[:], og[:], sg[:], ALU.mult, [og.b, sg.b], [o16.b])
                k.dma("sp", Oscr[sl, h * 512:(h + 1) * 512], o16[:], [o16.b], [Oscr.b], Oscr.b)
            if (not full) or n < NT - 1:
                for d in range(2):
                    ub = banks[7] if d == 0 else banks[3]
                    k.mm(ub[:, :], kz[:, n, d * P:(d + 1) * P], Vt[:, n, :], True, True, [kz.b, Vt.b], [ub.b])
                    k.stt("dve", state[:, d, :], state[:, d, :], float(RET_G[h] ** P), ub[:, :], ALU.mult, ALU.add,
                          [state.b, ub.b], [state.b])
                k.copy("act", stbf[:], state[:], [state.b], [stbf.b])
        if not full:
            k.dma("sp", Eo[h].rearrange("(d p) v -> p d v", p=P), state[:], [state.b], [Eo.b], Eo.b)
    if full:
        ot = k.sb("ot", [P, 2048], BF16, dma=True)
        oT = k.sb("oT", [P, 16, P], BF16)
        wch = [k.sb("wch%d" % i, [P, D], BF16, dma=True) for i in range(2)]
        yt = k.sb("yt", [P, D], F32)
        nw = 0
        for n in range(NT):
            sl = slice(n * P, (n + 1) * P)
            xt = xt2[n % 2]
            k.load("sp", xt, xt[:], x, x[sl, :])
            k.load("sp", ot, ot[:], Oscr, Oscr[sl, :])
            for r in range(2):
                for f in range(8):
                    k.tr(pb[:, f * P:(f + 1) * P], ot[:, (r * 8 + f) * P:(r * 8 + f + 1) * P], cm.ident[:],
                         [ot.b, cm.ident.b], [tb.b])
                k.copy("act", oT[:, r * 8:(r + 1) * 8, :], pb.rearrange("p (f t) -> p f t", f=8), [tb.b], [oT.b])
            for kk in range(16):
                wc = wch[nw % 2]
                nw += 1
                k.load("pool", wc, wc[:], w_out, w_out[0, kk * P:(kk + 1) * P, :])
                for half in range(2):
                    yb = banks[1 + half]
                    k.mm(yb[:, :], oT[:, kk, :], wc[:, half * 512:(half + 1) * 512], kk == 0, kk == 15,
                         [oT.b, wc.b], [yb.b])
            k.copy("act", yt[:, 0:512], banks[1][:, :], [banks[1].b], [yt.b])
            k.copy("dve", yt[:, 512:1024], banks[2][:, :], [banks[2].b], [yt.b])
            resid_update(k, cm, xt[:], xt.b, yt, G1)
            k.dma("sp", xo[sl, :], xt[:], [xt.b], [xo.b], xo.b)
    return k


class Ring(list):
    def __getitem__(self, t):
        return list.__getitem__(self, t % len(self))


def stage_ffn(S):
    T_ = S // NCORES
    NT = T_ // P
    k = K()
    x = k.dram("x", [T_, D], F32, "ExternalInput")
    mod = k.dram("mod", [4, 6 * D], F32, "ExternalInput")
    normg = k.dram("normg", [4, 4, D], F32, "ExternalInput")
    w1 = k.dram("w1", [1, D, DFF], F32, "ExternalInput")
    w2 = k.dram("w2", [1, DFF, D], F32, "ExternalInput")
    consts = k.dram("consts", [P, P], F32, "ExternalInput")
    xo = k.dram("xo", [T_, D], F32, "ExternalOutput")
    banks = k.psum_banks()
    cm = Common(k, consts)
    A2, B2, G3 = load_mod_vectors(k, mod, normg, 0, "f")
    xs = Ring(k.sb("x%d" % t, [P, D], F32, dma=True) for t in range(min(8, NT)))
    emit_ffn(k, cm, banks, xs, NT, w1, w2, A2, B2, G3, 0, xio=(x, xo))
    return k


def stage_mod():
    k = K()
    cvec = k.dram("cvec", [1, D], F32, "ExternalInput")
    aw = k.dram("aw", [4, D, 768], F32, "ExternalInput")
    ab = k.dram("ab", [4, 768], F32, "ExternalInput")
    modp = k.dram("modp", [4, 768], F32, "ExternalOutput")
    banks = k.psum_banks()
    row = k.sb("row", [1, D], F32, dma=True)
    k.load("sp", row, row[:], cvec, cvec[:, :])
    k.act(row[:], row[:], AF.Silu, [row.b], [row.b])
    one = k.sb("one", [1, 1], F32)
    k.memset("dve", one[:], 1.0, [one.b])
    cb = banks[0]
    for kt in range(8):
        k.mm(cb[:, kt:kt + 1], row[0:1, kt * P:(kt + 1) * P], one[0:1, 0:1], kt == 0, kt == 7, [row.b, one.b], [cb.b], skip=True)
    condc = k.sb("condc", [P, 8], F32)
    k.copy("dve", condc[:], cb[:, 0:8], [cb.b], [condc.b])
    W = [k.sb("W%d" % i, [P, 8, 768], F32, dma=True) for i in range(2)]
    bias = k.sb("bias", [1, 4, 768], F32, dma=True)
    k.load("sp", bias, bias[:], ab, ab.t.rearrange("(o l) n -> o l n", o=1))
    orow = k.sb("orow", [1, 4, 768], F32)
    for l in range(4):
        w = W[l % 2]
        k.load("sp", w, w[:], aw, aw[l].rearrange("(k p) n -> p k n", p=P))
        for c in range(2):
            bk = banks[1 + c]
            for kt in range(8):
                k.mm(bk[0:1, 0:384], condc[:, kt:kt + 1], w[:, kt, c * 384:(c + 1) * 384], kt == 0, kt == 7,
                     [condc.b, w.b], [bk.b])
            k.tt("dve", orow[:, l, c * 384:(c + 1) * 384], bk[0:1, 0:384], bias[:, l, c * 384:(c + 1) * 384], ALU.add,
                 [bk.b, bias.b], [orow.b])
    k.dma("sp", modp.t.rearrange("(o l) n -> o l n", o=1), orow[:], [orow.b], [modp.b], modp.b)
    return k


def _cyc(a, c, axis=0):
    a = np.moveaxis(a, axis, 0)
    S = a.shape[0]
    r = a.reshape(S // (NCORES * P), NCORES, P, *a.shape[1:])[:, c].reshape(S // NCORES, *a.shape[1:])
    return np.ascontiguousarray(np.moveaxis(r, 0, axis))


def _uncyc(parts, axis=0):
    parts = [np.moveaxis(np.asarray(p), axis, 0) for p in parts]
    T_ = parts[0].shape[0]
    a = np.stack([p.reshape(T_ // P, P, *p.shape[1:]) for p in parts], 1)
    a = a.reshape(T_ * NCORES, *parts[0].shape[1:])
    return np.ascontiguousarray(np.moveaxis(a, 0, axis))


def _cont(a, c):
    T_ = a.shape[0] // NCORES
    return np.ascontiguousarray(a[c * T_:(c + 1) * T_])


def kernel(x, c, positions, ada_w, ada_b, norm_g, ffn_w1, ffn_w2,
           mla_w_in, mla_q_norm, mla_kv_norm, mla_w_q_up, mla_w_kv_up, mla_w_out,
           sb_w_in, sb_w_out, ret_w_in, ret_gn_g, ret_w_out):
    f32 = lambda a: np.ascontiguousarray(np.asarray(a, dtype=np.float32))
    x = f32(x)
    S = x.shape[1]
    NT = S // NCORES // P
    cur = x[0]
    pos = np.ascontiguousarray(np.asarray(positions, dtype=np.int32)[0])
    ada_w, ada_b, norm_g = f32(ada_w), f32(ada_b), f32(norm_g)
    ffn_w1, ffn_w2 = f32(ffn_w1), f32(ffn_w2)
    consts = np.eye(P, dtype=np.float32)
    cores = range(NCORES)
    f64 = (10000.0 ** (-np.arange(0, 64, 2, dtype=np.float32) / 64)).astype(np.float32)
    invf64 = np.concatenate([f64, f64]).reshape(64, 1)
    invf128 = (10000.0 ** (-np.arange(0, 256, 2, dtype=np.float32) / 256)).astype(np.float32).reshape(P, 1)
    ntri = -(np.arange(P)[:, None] >= np.arange(P)[None, :]).astype(np.float32)

    res = run_spmd(stage_mod(), [dict(cvec=f32(c), aw=np.ascontiguousarray(ada_w[:, :, i * 768:(i + 1) * 768]),
                                      ab=np.ascontiguousarray(ada_b[:, i * 768:(i + 1) * 768])) for i in cores])
    mod = np.concatenate([np.asarray(r["modp"]) for r in res], axis=1)

    def rolled(layer):
        return np.roll(mod, -layer, axis=0), np.roll(norm_g, -layer, axis=0)

    def ffn(cur, layer):
        m, g = rolled(layer)
        res = run_spmd(stage_ffn(S), [dict(x=_cont(cur, i), mod=m, normg=g, w1=ffn_w1[layer:layer + 1],
                                           w2=ffn_w2[layer:layer + 1], consts=consts) for i in cores])
        return np.concatenate([np.asarray(r["xo"]) for r in res], axis=0)

    def mla(cur, layer, j):
        m, g = rolled(layer)
        wkv = f32(mla_w_kv_up)[j:j + 1]
        res = run_spmd(stage_mla_pre(S, 0, 0), [dict(
            x=_cyc(cur, i), pos=_cyc(pos, i), mod=m, normg=g, w_in=f32(mla_w_in)[j:j + 1],
            qng=f32(mla_q_norm)[j:j + 1], kvg=f32(mla_kv_norm)[j:j + 1], wqup=f32(mla_w_q_up)[j:j + 1],
            wkvup=wkv, consts=consts, invf64=invf64) for i in cores])
        KcTg = _uncyc([r["KcT"] for r in res], 1)
        KpTg = _uncyc([r["KpT"] for r in res], 1)
        Vcg = _uncyc([r["Vc"] for r in res], 0)
        res2 = run_spmd(stage_mla_attn(S, 0, 0), [dict(
            x=_cyc(cur, i), QT=np.asarray(res[i]["QT"]), KcTg=KcTg, KpTg=KpTg, Vcg=Vcg, mask=mla_masks(i),
            mod=m, normg=g, wkvup=wkv, w_out=f32(mla_w_out)[j:j + 1], consts=consts) for i in cores])
        return _uncyc([r["xo"] for r in res2], 0)

    def sbl(cur, layer):
        m, g = rolled(layer)
        res = run_spmd(stage_sb_pre(S, 0), [dict(x=_cyc(cur, i), mod=m, normg=g, w_in=f32(sb_w_in), consts=consts)
                                            for i in cores])
        KTg = _uncyc([r["KT"] for r in res], 1)
        Vg = _uncyc([r["V"] for r in res], 0)
        res2 = run_spmd(stage_sb_attn(S, 0), [dict(
            x=_cyc(cur, i), QT=np.asarray(res[i]["QT"]), KTg=KTg, Vg=Vg, mask=mla_masks(i, True, 1), ntri=ntri,
            mod=m, normg=g, w_out=f32(sb_w_out), consts=consts) for i in cores])
        return _uncyc([r["xo"] for r in res2], 0)

    def ret(cur, layer):
        m, g = rolled(layer)
        rcs = [ret_consts(i, NT) for i in cores]
        base = [dict(x=_cont(cur, i), pos=_cont(pos, i), mod=m, normg=g, w_in=f32(ret_w_in), consts=consts,
                     invf128=invf128, zeta=rcs[i]["zeta"]) for i in cores]
        res = run_spmd(stage_ret(S, 0, False), base)
        Eall = np.stack([np.asarray(r["E"]) for r in res])
        for i in cores:
            base[i].update(gng=f32(ret_gn_g), w_out=f32(ret_w_out), decayT=rcs[i]["decayT"], xi=rcs[i]["xi"],
                           coef=rcs[i]["coef"], Eall=Eall)
        res2 = run_spmd(stage_ret(S, 0, True), base)
        return np.concatenate([np.asarray(r["xo"]) for r in res2], axis=0)

    depth = ada_w.shape[0]
    for layer in range(depth):
        kind, j = layer % 3, layer // 3
        if kind == 0:
            cur = mla(cur, layer, j)
        elif kind == 1:
            cur = sbl(cur, layer)
        else:
            cur = ret(cur, layer)
        cur = ffn(cur, layer)
    return np.ascontiguousarray(cur[None].astype(np.float32))
```
